# Optimizing a Trainium2 kernel written in Bass

```python
import jax, jax.numpy as jnp
from jax import lax
import numpy as np

D_MODEL = 2048
BATCH = 2
SEQ = 8192
DEPTH = 4
DEC_BATCH = 1
DEC_SEQ = 16384
PAST_LEN = 128

N_MIXERS = 2
N_POOL_LAYERS = (DEPTH + 1) // 2
N_SGU_LAYERS = DEPTH // 2
POOL_WINDOWS = (2, 4, 8, 16)
N_POOL_GROUPS = len(POOL_WINDOWS)
POOL_GROUP = D_MODEL // N_POOL_GROUPS
SGU_CHUNK = 128
SGU_FF = 6 * D_MODEL
SGU_HALF = SGU_FF // 2
N_SGU_HEADS = 8
SGU_HEAD_DIM = SGU_HALF // N_SGU_HEADS
D_FF = 4 * D_MODEL
N_MOD = 6
EPS = 1e-6

kernel_name = "hybrid_pool_sgu_adaln_encoder"


def _rmsnorm(x, g):
    xf = x.astype(jnp.float32)
    y = xf * lax.rsqrt(jnp.mean(xf * xf, axis=-1, keepdims=True) + EPS)
    return (y * g.astype(jnp.float32)).astype(x.dtype)


def _modulate(h, shift, scale):
    return h * (1 + scale[:, None, :]) + shift[:, None, :]


def _pool_mixer(h, w_in, w_grp, scale, w_out):
    z = jnp.einsum('bsd,de->bse', h, w_in)
    S = z.shape[1]
    zf = z.astype(jnp.float32)
    cs = jnp.pad(jnp.cumsum(zf, axis=1), ((0, 0), (1, 0), (0, 0)))
    pos = jnp.arange(S)
    outs = []
    for g, w in enumerate(POOL_WINDOWS):
        sl = slice(g * POOL_GROUP, (g + 1) * POOL_GROUP)
        lo = jnp.maximum(pos - w // 2, 0)
        hi = jnp.minimum(pos + w // 2 - 1, S - 1) + 1
        csg = cs[:, :, sl]
        cnt = (hi - lo).astype(jnp.float32)[None, :, None]
        mean = (csg[:, hi] - csg[:, lo]) / cnt
        diff = (mean - zf[:, :, sl]).astype(h.dtype)
        outs.append(jnp.einsum('bsc,ce->bse', diff, w_grp[g]))
    y = jnp.concatenate(outs, axis=-1) * scale
    return jnp.einsum('bsd,de->bse', y, w_out)


def _sgu_mixer(h, w_in, v_gain, w_s, b_s, w_out):
    z = jax.nn.gelu(jnp.einsum('bsd,df->bsf', h, w_in), approximate=False)
    u, v = jnp.split(z, 2, axis=-1)
    v = _rmsnorm(v, v_gain)
    B, S, _ = v.shape
    n_chunks = S // SGU_CHUNK
    vc = v.reshape(B, n_chunks, SGU_CHUNK, N_SGU_HEADS, SGU_HEAD_DIM)
    mixed = jnp.einsum('hpq,bnqhd->bnphd', w_s, vc) + b_s.T[None, None, :, :, None]
    gated = u * mixed.reshape(B, S, SGU_HALF)
    return jnp.einsum('bsf,fd->bsd', gated, w_out)


def _channel_mlp(h, w1, w2):
    a = jnp.maximum(jnp.einsum('bsd,df->bsf', h, w1), 0)
    return jnp.einsum('bsf,fd->bsd', a * a, w2)


def _trunk(x, c, norm1_g, norm2_g, mod_w, mod_b,
           pool_w_in, pool_w_grp, pool_scale, pool_w_out,
           sgu_w_in, sgu_v_gain, sgu_w_s, sgu_b_s, sgu_w_out,
           mlp_w1, mlp_w2, final_g):
    c_act = jax.nn.silu(c)
    for i in range(DEPTH):
        mod = jnp.einsum('bd,de->be', c_act, mod_w[i]) + mod_b[i]
        sh1, sc1, g1, sh2, sc2, g2 = jnp.split(mod, N_MOD, axis=-1)
        h = _modulate(_rmsnorm(x, norm1_g[i]), sh1, sc1)
        j = i // N_MIXERS
        if i % N_MIXERS == 0:
            y = _pool_mixer(h, pool_w_in[j], pool_w_grp[j], pool_scale[j], pool_w_out[j])
        else:
            y = _sgu_mixer(h, sgu_w_in[j], sgu_v_gain[j], sgu_w_s[j], sgu_b_s[j], sgu_w_out[j])
        x = x + g1[:, None, :] * y
        h = _modulate(_rmsnorm(x, norm2_g[i]), sh2, sc2)
        x = x + g2[:, None, :] * _channel_mlp(h, mlp_w1[i], mlp_w2[i])
    return _rmsnorm(x, final_g)


def setup_inputs(seed: int = 0) -> dict:
    key = jax.random.key(seed)
    ks = jax.random.split(key, 24)
    f32 = jnp.float32
    D = D_MODEL

    def nrm(k, shape, scale):
        return jax.random.normal(k, shape, f32) * scale

    return {
        "x_prompt": nrm(ks[0], (BATCH, SEQ, D), 1.0),
        "x_sample": nrm(ks[1], (DEC_BATCH, DEC_SEQ, D), 1.0),
        "c_prompt": nrm(ks[2], (BATCH, D), 1.0),
        "c_sample": nrm(ks[3], (DEC_BATCH, D), 1.0),
        "norm1_g": 1.0 + nrm(ks[4], (DEPTH, D), 0.05),
        "norm2_g": 1.0 + nrm(ks[5], (DEPTH, D), 0.05),
        "mod_w": nrm(ks[6], (DEPTH, D, N_MOD * D), 0.5 * D ** -0.5),
        "mod_b": nrm(ks[7], (DEPTH, N_MOD * D), 0.01),
        "pool_w_in": nrm(ks[8], (N_POOL_LAYERS, D, D), D ** -0.5),
        "pool_w_grp": nrm(ks[9], (N_POOL_LAYERS, N_POOL_GROUPS, POOL_GROUP, POOL_GROUP), POOL_GROUP ** -0.5),
        "pool_scale": 1.0 + nrm(ks[10], (N_POOL_LAYERS, D), 0.1),
        "pool_w_out": nrm(ks[11], (N_POOL_LAYERS, D, D), D ** -0.5),
        "sgu_w_in": nrm(ks[12], (N_SGU_LAYERS, D, SGU_FF), D ** -0.5),
        "sgu_v_gain": 1.0 + nrm(ks[13], (N_SGU_LAYERS, SGU_HALF), 0.05),
        "sgu_w_s": nrm(ks[14], (N_SGU_LAYERS, N_SGU_HEADS, SGU_CHUNK, SGU_CHUNK), SGU_CHUNK ** -0.5),
        "sgu_b_s": 1.0 + nrm(ks[15], (N_SGU_LAYERS, N_SGU_HEADS, SGU_CHUNK), 0.1),
        "sgu_w_out": nrm(ks[16], (N_SGU_LAYERS, SGU_HALF, D), SGU_HALF ** -0.5),
        "mlp_w1": nrm(ks[17], (DEPTH, D, D_FF), D ** -0.5),
        "mlp_w2": nrm(ks[18], (DEPTH, D_FF, D), D_FF ** -0.5),
        "final_g": 1.0 + nrm(ks[19], (D,), 0.05),
    }


def reference(x_prompt, x_sample, c_prompt, c_sample, norm1_g, norm2_g, mod_w, mod_b,
              pool_w_in, pool_w_grp, pool_scale, pool_w_out,
              sgu_w_in, sgu_v_gain, sgu_w_s, sgu_b_s, sgu_w_out,
              mlp_w1, mlp_w2, final_g):
    y_prompt = _trunk(x_prompt, c_prompt, norm1_g, norm2_g, mod_w, mod_b,
                      pool_w_in, pool_w_grp, pool_scale, pool_w_out,
                      sgu_w_in, sgu_v_gain, sgu_w_s, sgu_b_s, sgu_w_out,
                      mlp_w1, mlp_w2, final_g)
    y_sample = _trunk(x_sample, c_sample, norm1_g, norm2_g, mod_w, mod_b,
                      pool_w_in, pool_w_grp, pool_scale, pool_w_out,
                      sgu_w_in, sgu_v_gain, sgu_w_s, sgu_b_s, sgu_w_out,
                      mlp_w1, mlp_w2, final_g)
    return (y_prompt, y_sample)
```

```python
import numpy as np
import concourse.bass as bass
import concourse.mybir as mybir
from concourse.bass_utils import run_bass_kernel_spmd

F32 = mybir.dt.float32
BF16 = mybir.dt.bfloat16
AF = mybir.ActivationFunctionType
ALU = mybir.AluOpType
AX = mybir.AxisListType

D = 2048
KC = 16
DEPTH = 4
EPS = 1e-6
NCORE = 8
TOK = 4096
PADL = 136
XT = TOK + 2 * PADL
NCH = 5
TILES = [(-1, 4), (4, 9), (9, 14), (14, 19), (19, 24), (24, 29), (29, 33)]
NSLOT = 4
NXSLOT = 7
HID = 8192
SF = 12288
SH = 6144


class Eng:
    def __init__(self, K, e, name):
        self.K, self.e, self.name = K, e, name
        self.gen = 0
        self.waited = {}
        self.new_sem()

    def new_sem(self):
        self.sem = self.K.sem(f"{self.name}{self.gen}")
        self.gen += 1
        self.n = 0

    def wait(self, tk):
        sem, val, nm = tk
        if self.waited.get(nm, 0) >= val:
            return
        self.e.wait_ge(sem, val)
        self.waited[nm] = val

    def done(self, ins):
        self.n += 1
        ins.then_inc(self.sem, 1)
        return (self.sem, self.n, self.sem_name())

    def sem_name(self):
        return f"{self.name}{self.gen - 1}"


class Kern:
    def __init__(self):
        self.nc = bass.Bass("TRN2", target_bir_lowering=False)
        self._keep = []
        self.W = {}
        self.R = {}
        nc = self.nc
        self.pe = Eng(self, nc.tensor, "pe")
        self.act = Eng(self, nc.scalar, "act")
        self.dve = Eng(self, nc.vector, "dve")
        self.pool = Eng(self, nc.gpsimd, "pool")
        self.sp = Eng(self, nc.sync, "sp")
        self.own = {}
        self.sp_sems = [[self.sem(f"spd{i}"), 0, f"spd{i}", None] for i in range(8)]
        self.sp_i = 0
        self.slot_sems = [[self.sem(f"slot{i}"), 0, f"slot{i}"] for i in range(NSLOT + NXSLOT)]
        self.slot_i = 0
        self.nslot_active = NSLOT
        self.ps_i = 0
        self.last = {}

    def sem(self, name):
        g = self.nc.semaphore(name)
        s = g.__enter__()
        self._keep.append(g)
        return s

    def sb(self, name, shape, dt):
        g = self.nc.sbuf_tensor(name, shape, dt)
        t = g.__enter__()
        self._keep.append(g)
        return t

    def psum(self, name, shape, dt):
        g = self.nc.psum_tensor(name, shape, dt)
        t = g.__enter__()
        self._keep.append(g)
        return t

    def dram(self, name, shape, kind="ExternalInput"):
        return self.nc.dram_tensor(name, list(shape), F32, kind=kind).ap()

    def pre(self, eng, reads, writes):
        for k in reads:
            for tk in self.W.get(k, {}).values():
                eng.wait(tk)
        for k in writes:
            for tk in self.W.get(k, {}).values():
                if tk[2].startswith(eng.name) and eng.name in ("pe", "act", "dve"):
                    continue
                eng.wait(tk)
            for tk in self.R.get(k, {}).values():
                if tk[2].startswith(eng.name) and eng.name in ("pe", "act", "dve"):
                    continue
                eng.wait(tk)

    def post(self, tk, reads, writes):
        for k in reads:
            self.R.setdefault(k, {})[tk[2]] = tk
        for k in writes:
            self.W[k] = {tk[2]: tk}
            self.R[k] = {}
        self.last[tk[2]] = tk

    def op(self, eng, reads, writes, fn):
        self.pre(eng, reads, writes)
        tk = eng.done(fn())
        self.post(tk, reads, writes)
        return tk

    def A(self, reads, writes, fn):
        return self.op(self.act, reads, writes, fn)

    def V(self, reads, writes, fn):
        return self.op(self.dve, reads, writes, fn)

    def barrier(self):
        tks = [tk for nm, tk in self.last.items() if nm[:2] in ("pe", "ac", "dv")]
        for eng in (self.pe, self.act, self.dve):
            for tk in tks:
                if tk[2] == eng.sem_name():
                    continue
                eng.wait(tk)
        self.W["__all__"] = {tk[2]: tk for tk in tks}

    def sp_dma(self, out, in_, reads, writes):
        reads = list(reads)
        ent = self.sp_sems[self.sp_i]
        self.sp_i = (self.sp_i + 1) % len(self.sp_sems)
        if ent[3] is not None:
            self.sp.wait(ent[3])
        self.pre(self.sp, list(reads) + ["__all__"], writes)
        ent[1] += 16
        self.nc.sync.dma_start(out=out, in_=in_).then_inc(ent[0], 16)
        tk = (ent[0], ent[1], ent[2])
        ent[3] = tk
        self.post(tk, reads, writes)
        return tk

    def wload(self, shape, src):
        i = self.slot_i
        self.slot_i = (self.slot_i + 1) % self.nslot_active
        ent = self.slot_sems[i]
        key = f"ring{i}"
        self.pre(self.pool, [], [key])
        n = 1
        for s in shape[1:]:
            n *= s
        view = self.slot_views[i][:, 0:n]
        if len(shape) == 3:
            view = view.rearrange("p (a b) -> p a b", b=shape[2])
        ent[1] += 16
        self.nc.gpsimd.dma_start(out=view, in_=src).then_inc(ent[0], 16)
        tk = (ent[0], ent[1], ent[2])
        self.post(tk, [], [key])
        return view, key

    def mm(self, out, pairs, reads, writes):
        self.pre(self.pe, reads, writes)
        n = len(pairs)
        ins = None
        for i, (l, r) in enumerate(pairs):
            ins = self.nc.tensor.matmul(out, l, r, start=(i == 0), stop=(i == n - 1))
        tk = self.pe.done(ins)
        self.post(tk, reads, writes)
        return tk

    def bank(self):
        i = self.ps_i
        self.ps_i = (self.ps_i + 1) % 8
        return self.ps[i], f"ps{i}"


def nblocks(c0, c1):
    n = (c1 - c0) // 128
    nb = (n + 2) // 3
    base, rem = divmod(n, nb)
    out = []
    c = c0
    for i in range(nb):
        w = (base + (1 if i < rem else 0)) * 128
        out.append((c, c + w))
        c += w
    return out


def build():
    K = Kern()
    nc = K.nc
    xT = K.dram("xT", [D, XT])
    mr = K.dram("mr", [128, 5, XT])
    xh = K.dram("xh", [128, len(TILES), KC * 16])
    cc = K.dram("cc", [128, KC])
    n1 = K.dram("n1", [128, DEPTH, KC])
    n2 = K.dram("n2", [128, DEPTH, KC])
    mb = K.dram("mb", [128, DEPTH, 96])
    psc = K.dram("psc", [128, 2, KC])
    vg = K.dram("vg", [128, 2, 48])
    fg = K.dram("fg", [128, KC])
    wst = K.dram("wst", [128, 2, 8, 128])
    bbd = K.dram("bbd", [128, 2, 8, 128])
    mod_w = K.dram("mod_w", [DEPTH, D, 6 * D])
    pool_w_in = K.dram("pool_w_in", [2, D, D])
    pool_w_grp = K.dram("pool_w_grp", [2, 4, 512, 512])
    pool_w_out = K.dram("pool_w_out", [2, D, D])
    sgu_w_in = K.dram("sgu_w_in", [2, D, SF])
    sgu_w_out = K.dram("sgu_w_out", [2, SH, D])
    mlp_w1 = K.dram("mlp_w1", [DEPTH, D, HID])
    mlp_w2 = K.dram("mlp_w2", [DEPTH, HID, D])
    yT = K.dram("yT", [D, TOK], kind="ExternalOutput")
    xTv = xT.rearrange("(k p) t -> p k t", p=128)
    yTv = yT.rearrange("(k p) t -> p k t", p=128)

    WMAX = NCH * 128
    X = K.sb("X", [128, KC, (NCH + 1) * 128], F32)
    H = K.sb("H", [128, KC, WMAX], BF16)
    Hh = K.sb("Hh", [128, KC, 16], BF16)
    XH = K.sb("XH", [128, KC, 16], F32)
    XH0 = K.sb("XH0", [128, KC, 16], F32)
    XL = K.sb("XL", [128, 2, KC, 8], F32)
    BIG = K.sb("BIG", [128, NCH * SH // 2], F32)
    BIGb = BIG[:, :].bitcast(BF16)
    Vv = BIGb.rearrange("p (t c) -> p t c", c=SH)
    Abuf = BIGb[:, 0:KC * WMAX].rearrange("p (k w) -> p k w", w=WMAX)
    SQ = BIGb[:, 10240:10240 + KC * 384].rearrange("p (k w) -> p k w", w=384)
    LZ = WMAX + 16
    NT = BIG[:, 8192:8960].rearrange("p (a b) -> p a b", b=384)
    RS = BIG[:, 8960:9344]
    SR = BIG[:, 9344:9728]
    RT = BIG[:, 9728:10496].rearrange("p (a b) -> p a b", b=384)
    ZE = BIG[:, 9728:9728 + LZ]
    T1 = BIG[:, 10384:10384 + LZ]
    T2 = BIG[:, 11040:11040 + LZ]
    MR = BIG[:, 11696:11696 + 5 * LZ].rearrange("p (a b) -> p a b", b=LZ)
    G = K.sb("G", [128, 2, 6, WMAX], BF16)
    U = K.sb("U", [128, 2, 2, WMAX], BF16)
    WSR = K.sb("WSR", [128, 2, NCH, 128], BF16)
    GT = K.sb("GT", [128, 2, 384], F32)
    BB = K.sb("BB", [128, 8, 128], F32)
    WST = K.sb("WST", [128, 8, 128], F32)
    SSQ = K.sb("SSQ", [128, NCH, 24], F32)
    SQV = K.sb("SQV", [128, 2, 256], F32)
    RV = K.sb("RV", [128, 2, NCH], F32)
    K.ring = K.sb("ring", [128, NSLOT, 4096], BF16)
    K.slot_views = [K.ring[:, i, :] for i in range(NSLOT)] + [BIGb[:, i * 4096:(i + 1) * 4096] for i in range(NXSLOT)]
    SQh = K.sb("SQh", [128, KC, 16], BF16)
    RSh = K.sb("RSh", [128, 2, 16], F32)
    ONES = K.sb("ONES", [128, 128], BF16)
    CA = K.sb("CA", [128, KC], F32)
    CAb = K.sb("CAb", [128, KC], BF16)
    MODC = K.sb("MODC", [128, DEPTH, 96], F32)
    MBt = K.sb("MBt", [128, DEPTH, 96], F32)
    N1 = K.sb("N1", [128, DEPTH, KC], F32)
    N2 = K.sb("N2", [128, DEPTH, KC], F32)
    GM = K.sb("GM", [128, DEPTH, 2, KC], F32)
    PSC = K.sb("PSC", [128, 2, KC], F32)
    VG = K.sb("VG", [128, 2, 48], F32)
    FG = K.sb("FG", [128, KC], F32)
    K.ps = [K.psum(f"ps{i}", [128, 512], F32) for i in range(8)]

    SQD = float(np.sqrt(D))

    K.sp_dma(CA[:], cc, [], ["CA"])
    K.sp_dma(MBt[:], mb, [], ["MBt"])
    K.sp_dma(N1[:], n1, [], ["N1"])
    K.sp_dma(N2[:], n2, [], ["N2"])
    K.sp_dma(PSC[:], psc, [], ["PSC"])
    K.sp_dma(VG[:], vg, [], ["VG"])
    K.sp_dma(FG[:], fg, [], ["FG"])
    K.V([], ["ONES"], lambda: nc.vector.memset(ONES[:], 1.0))
    K.A(["CA"], ["CAb"], lambda: nc.scalar.activation(out=CAb[:], in_=CA[:], func=AF.Silu))
    K.nslot_active = NSLOT + NXSLOT
    for l in range(DEPTH):
        pt, pk = K.bank()
        mwv = mod_w[l].rearrange("(k p) e -> p k e", p=128)
        for b in range(48):
            wv, wk = K.wload([128, KC, 256], mwv[:, :, b * 256:(b + 1) * 256])
            for e2 in range(2):
                j = b * 2 + e2
                K.mm(pt[:, j:j + 1],
                     [(wv[:, k, e2 * 128:(e2 + 1) * 128], CAb[:, k:k + 1]) for k in range(KC)],
                     [wk, "CAb"], [pk])
        K.V([pk, "MBt"], ["MODC"],
            lambda: nc.vector.tensor_tensor(out=MODC[:, l, :], in0=pt[:, 0:96], in1=MBt[:, l, :], op=ALU.add))
    for l in range(DEPTH):
        for s, Nn in ((0, N1), (1, N2)):
            sc = MODC[:, l, (3 * s + 1) * KC:(3 * s + 2) * KC]
            K.V(["MODC"], ["GMt"],
                lambda: nc.vector.tensor_scalar(out=GM[:, l, s, :], in0=sc, scalar1=1.0, scalar2=SQD,
                                                op0=ALU.add, op1=ALU.mult))
            K.V(["GMt", "N1", "N2"], ["GM"],
                lambda: nc.vector.tensor_tensor(out=GM[:, l, s, :], in0=GM[:, l, s, :], in1=Nn[:, l, :],
                                                op=ALU.mult))
    K.V(["FG"], ["FGs"], lambda: nc.vector.tensor_scalar(out=FG[:], in0=FG[:], scalar1=SQD, scalar2=None,
                                                         op0=ALU.mult))
    K.barrier()
    K.nslot_active = NSLOT
    K.slot_i = 0

    def modv(l, idx):
        return MODC[:, l, idx * KC:(idx + 1) * KC]

    SQ2 = BIGb[:, 23392:23392 + KC * 256].rearrange("p (k w) -> p k w", w=256)
    RS2 = BIG[:, 13744:14000]
    SR2 = BIG[:, 14000:14256]
    NBUF = [(SQ, RS, SR, "SQ", "RS", "SR"), (SQ2, RS2, SR2, "SQ2", "RS2", "SR2")]
    HBUF = (SQh, RSh[:, 0, :], RSh[:, 1, :], "SQh", "RSh", "SRh")

    def rstd_blocks(items, key_src):
        for (src, nb, B) in items:
            if B[3] == "SQ2":
                K.V(key_src, [B[3]], lambda: nc.vector.tensor_tensor(out=B[0][:, :, 0:nb], in0=src, in1=src,
                                                                     op=ALU.mult))
            else:
                K.A(key_src, [B[3]], lambda: nc.scalar.activation(out=B[0][:, :, 0:nb], in_=src, func=AF.Square))
        banks = []
        for (src, nb, B) in items:
            pt, pk = K.bank()
            K.mm(pt[:, 0:nb], [(ONES[:], B[0][:, k, 0:nb]) for k in range(KC)], [B[3], "ONES"], [pk])
            banks.append((pt, pk))
        for (src, nb, B), (pt, pk) in zip(items, banks):
            K.A([pk], [B[5]], lambda: nc.scalar.activation(out=B[2][:, 0:nb], in_=pt[:, 0:nb], func=AF.Sqrt,
                                                           bias=float(D * EPS), scale=1.0))
            K.V([B[5]], [B[4]], lambda: nc.vector.reciprocal(out=B[1][:, 0:nb], in_=B[2][:, 0:nb]))

    def modulate(src, dst, nb, B, gm, sh, key_src, key_dst):
        for k in range(KC):
            tb = k % 2
            K.V(key_src + [B[4]], [f"NT{tb}"],
                lambda: nc.vector.tensor_tensor(out=NT[:, tb, 0:nb], in0=src[:, k, :], in1=B[1][:, 0:nb],
                                                op=ALU.mult))
            K.A([f"NT{tb}"], key_dst,
                lambda: nc.scalar.activation(out=dst[:, k, :], in_=NT[:, tb, 0:nb], func=AF.Identity,
                                             bias=sh[:, k:k + 1], scale=gm[:, k:k + 1]))

    def norm_range(c0, c1, gm, sh, halo=None):
        blks = nblocks(c0, c1)
        items = [(X[:, :, a:b], b - a, NBUF[i]) for i, (a, b) in enumerate(blks)]
        if halo is not None:
            rstd_blocks([(halo[0][:, :, :], 16, HBUF)], [halo[1]])
        rstd_blocks(items, ["X"])
        if halo is not None:
            modulate(halo[0][:, :, :], Hh[:, :, :], 16, HBUF, gm, sh, [halo[1]], ["Hh"])
        for (a, b), (src, nb, B) in zip(blks, items):
            modulate(src, H[:, :, a - c0:b - c0], nb, B, gm, sh, ["X"], [f"H{blks.index((a, b))}"])

    def resid_update(pt, pk, e, a, b, gate):
        K.V([pk, "X"], ["X"],
            lambda: nc.vector.scalar_tensor_tensor(out=X[:, e, a:b], in0=pt[:, 0:b - a], scalar=gate[:, e:e + 1],
                                                   in1=X[:, e, a:b], op0=ALU.mult, op1=ALU.add))

    def mlp(l, c0, c1):
        W = c1 - c0
        norm_range(c0, c1, GM[:, l, 1, :], modv(l, 3))
        g2 = modv(l, 5)
        blks = nblocks(c0, c1)
        w1v = mlp_w1[l].rearrange("(k p) f -> p k f", p=128)
        for s in range(4):
            for b in range(8):
                f0 = s * 2048 + b * 256
                wv, wk = K.wload([128, KC, 256], w1v[:, :, f0:f0 + 256])
                order = [(f2, ab) for f2 in range(2) for ab in blks]
                if s == 0 and b == 0:
                    order = [(f2, ab) for ab in blks for f2 in range(2)]
                for (f2, (a, bb_)) in order:
                    fi = b * 2 + f2
                    nb = bb_ - a
                    pt, pk = K.bank()
                    K.mm(pt[:, 0:nb],
                         [(wv[:, k, f2 * 128:(f2 + 1) * 128], H[:, k, a - c0:bb_ - c0]) for k in range(KC)],
                         [wk, f"H{blks.index((a, bb_))}"], [pk])
                    tb = K.ps_i % 2
                    K.A([pk], [f"RT{tb}"],
                        lambda: nc.scalar.activation(out=RT[:, tb, 0:nb], in_=pt[:, 0:nb], func=AF.Relu))
                    K.V([f"RT{tb}"], ["A"],
                        lambda: nc.vector.tensor_tensor(out=Abuf[:, fi, a - c0:bb_ - c0], in0=RT[:, tb, 0:nb],
                                                        in1=RT[:, tb, 0:nb], op=ALU.mult))
            w2v = mlp_w2[l][s * 2048:(s + 1) * 2048, :].rearrange("(f p) e -> p f e", p=128)
            for b in range(8):
                wv, wk = K.wload([128, KC, 256], w2v[:, :, b * 256:(b + 1) * 256])
                for e2 in range(2):
                    e = b * 2 + e2
                    for (a, bb_) in blks:
                        nb = bb_ - a
                        pt, pk = K.bank()
                        K.mm(pt[:, 0:nb],
                             [(wv[:, f, e2 * 128:(e2 + 1) * 128], Abuf[:, f, a - c0:bb_ - c0]) for f in range(KC)],
                             [wk, "A"], [pk])
                        resid_update(pt, pk, e, a, bb_, g2)

    def pool_layer(l, j, c0, c1, tok0, halo):
        W = c1 - c0
        L = W + 16
        gm, sh, g1 = GM[:, l, 0, :], modv(l, 0), modv(l, 2)
        norm_range(c0, c1, gm, sh, halo=halo)
        K.sp_dma(MR[:, :, 0:L], mr[:, :, tok0 - 8 + PADL: tok0 + W + 8 + PADL], [], ["MR", "SQ2", "RS2", "SR2"])
        blks = nblocks(c0, c1)
        wiv = pool_w_in[j].rearrange("(k p) e -> p k e", p=128)
        for e in reversed(range(KC)):
            if e % 2 == 1:
                wv, wk = K.wload([128, KC, 256], wiv[:, :, (e - 1) * 128:(e + 1) * 128])
            e2 = e % 2
            for (a, b) in blks:
                nb = b - a
                pt, pk = K.bank()
                K.mm(pt[:, 0:nb], [(wv[:, k, e2 * 128:(e2 + 1) * 128], H[:, k, a - c0:b - c0]) for k in range(KC)],
                     [wk, f"H{blks.index((a, b))}"], [pk])
                K.A([pk], ["ZE"], lambda: nc.scalar.copy(out=ZE[:, 8 + a - c0: 8 + b - c0], in_=pt[:, 0:nb]))
            pt, pk = K.bank()
            K.mm(pt[:, 0:16], [(wv[:, k, e2 * 128:(e2 + 1) * 128], Hh[:, k, :]) for k in range(KC)],
                 [wk, "Hh"], [pk])
            K.A([pk], ["ZE"], lambda: nc.scalar.copy(out=ZE[:, 0:8], in_=pt[:, 0:8]))
            K.A([pk], ["ZE"], lambda: nc.scalar.copy(out=ZE[:, 8 + W:16 + W], in_=pt[:, 8:16]))
            K.V(["ZE", "MR"], ["ZE"],
                lambda: nc.vector.tensor_tensor(out=ZE[:, 0:L], in0=ZE[:, 0:L], in1=MR[:, 0, 0:L], op=ALU.mult))
            g = e // 4
            src, srck = ZE, "ZE"
            bufs = [(T1, "T1"), (T2, "T2")]
            for st in range(g + 1):
                sh_ = 1 << st
                dst, dstk = bufs[st % 2]
                n = L - (2 * sh_ - 1)
                K.V([srck], [dstk],
                    lambda: nc.vector.tensor_tensor(out=dst[:, 0:n], in0=src[:, 0:n], in1=src[:, sh_:sh_ + n],
                                                    op=ALU.add))
                src, srck = dst, dstk
            T3, t3k = bufs[(g + 1) % 2]
            hw = (2 << g) // 2
            o = 8 - hw
            K.V([srck, "MR"], [t3k],
                lambda: nc.vector.tensor_tensor(out=T3[:, 0:W], in0=src[:, o:o + W], in1=MR[:, 1 + g, 8:8 + W],
                                                op=ALU.mult))
            K.V([t3k, "ZE"], ["A"],
                lambda: nc.vector.tensor_tensor(out=Abuf[:, e, 0:W], in0=T3[:, 0:W], in1=ZE[:, 8:8 + W],
                                                op=ALU.subtract))
        for g in range(4):
            wgv = pool_w_grp[j, g].rearrange("(k p) e -> p k e", p=128)
            wv, wk = K.wload([128, 4, 512], wgv)
            for eo in range(4):
                e = 4 * g + eo
                for (a, b) in blks:
                    nb = b - a
                    pt, pk = K.bank()
                    K.mm(pt[:, 0:nb],
                         [(wv[:, ki, eo * 128:(eo + 1) * 128], Abuf[:, 4 * g + ki, a - c0:b - c0]) for ki in range(4)],
                         [wk, "A"], [pk])
                    K.A([pk, f"H{blks.index((a, b))}"], [f"H{blks.index((a, b))}"],
                        lambda: nc.scalar.activation(out=H[:, e, a - c0:b - c0], in_=pt[:, 0:nb], func=AF.Identity,
                                                     bias=0.0, scale=PSC[:, j, e:e + 1]))
        wov = pool_w_out[j].rearrange("(k p) e -> p k e", p=128)
        for eb in range(8):
            wv, wk = K.wload([128, KC, 256], wov[:, :, eb * 256:(eb + 1) * 256])
            for e2 in range(2):
                e = eb * 2 + e2
                for (a, b) in blks:
                    nb = b - a
                    pt, pk = K.bank()
                    K.mm(pt[:, 0:nb],
                         [(wv[:, k, e2 * 128:(e2 + 1) * 128], H[:, k, a - c0:b - c0]) for k in range(KC)],
                         [wk, f"H{blks.index((a, b))}"], [pk])
                    resid_update(pt, pk, e, a, b, g1)

    def sgu_layer(l, j, c0, c1):
        W = c1 - c0
        ntc = W // 128
        gm, sh, g1 = GM[:, l, 0, :], modv(l, 0), modv(l, 2)
        K.sp_dma(WST[:], wst[:, j], [], ["WST"])
        K.sp_dma(BB[:], bbd[:, j], [], ["BB"])
        norm_range(c0, c1, gm, sh)
        K.barrier()
        blks = nblocks(c0, c1)
        wiv = sgu_w_in[j].rearrange("(k p) f -> p k f", p=128)
        for vb in range(24):
            wv, wk = K.wload([128, KC, 256], wiv[:, :, SH + vb * 256: SH + (vb + 1) * 256])
            for tc in range(ntc):
                pt, pk = K.bank()
                K.mm(pt[:, 0:256], [(H[:, k, tc * 128:(tc + 1) * 128], wv[:, k, :]) for k in range(KC)],
                     [wk, f"H{[i for i, (ba, bb2) in enumerate(blks) if ba <= c0 + tc * 128 < bb2][0]}"], [pk])
                K.A([pk], ["V"], lambda: nc.scalar.activation(out=Vv[:, tc, vb * 256:(vb + 1) * 256],
                                                              in_=pt[:, 0:256], func=AF.Gelu))
                tb = (vb * ntc + tc) % 2
                K.V(["V"], [f"SQV{tb}"],
                    lambda: nc.vector.tensor_tensor(out=SQV[:, tb, :], in0=Vv[:, tc, vb * 256:(vb + 1) * 256],
                                                    in1=Vv[:, tc, vb * 256:(vb + 1) * 256], op=ALU.mult))
                K.V([f"SQV{tb}"], ["SSQ"],
                    lambda: nc.vector.reduce_sum(out=SSQ[:, tc, vb:vb + 1], in_=SQV[:, tb, :], axis=AX.X))
        K.V(["SSQ"], ["RV0"], lambda: nc.vector.reduce_sum(out=RV[:, 0, 0:ntc], in_=SSQ[:, 0:ntc, :], axis=AX.X))
        K.A(["RV0"], ["RV1"], lambda: nc.scalar.activation(out=RV[:, 1, 0:ntc], in_=RV[:, 0, 0:ntc], func=AF.Sqrt,
                                                           bias=float(EPS), scale=1.0 / SH))
        K.V(["RV1"], ["RV"], lambda: nc.vector.reciprocal(out=RV[:, 0, 0:ntc], in_=RV[:, 1, 0:ntc]))
        wov = sgu_w_out[j].rearrange("(f p) e -> p f e", p=128)
        ui = 0

        def sgu_wout(hh):
            for ob in range(4):
                wv, wk = K.wload([128, 6, 512], wov[:, hh * 6:(hh + 1) * 6, ob * 512:(ob + 1) * 512])
                for e4 in range(4):
                    e = ob * 4 + e4
                    for (a, b) in blks:
                        nb = b - a
                        pt, pk = K.bank()
                        K.mm(pt[:, 0:nb],
                             [(wv[:, f, e4 * 128:(e4 + 1) * 128], G[:, hh % 2, f, a - c0:b - c0]) for f in range(6)],
                             [wk, f"G{hh % 2}"], [pk])
                        resid_update(pt, pk, e, a, b, g1)

        for h in range(8):
            hb = h % 2
            for tc in range(ntc):
                K.V(["RV", "WST"], [f"WSR{hb}"],
                    lambda: nc.vector.tensor_scalar(out=WSR[:, hb, tc, :], in0=WST[:, h, :],
                                                    scalar1=RV[:, 0, tc:tc + 1], scalar2=None, op0=ALU.mult))
            for ub in range(3):
                col0 = h * 768 + ub * 256
                wv, wk = K.wload([128, KC, 256], wiv[:, :, col0:col0 + 256])
                ubuf = ui % 2
                ui += 1
                for c2 in range(2):
                    for (a, b) in blks:
                        nb = b - a
                        pt, pk = K.bank()
                        K.mm(pt[:, 0:nb],
                             [(wv[:, k, c2 * 128:(c2 + 1) * 128], H[:, k, a - c0:b - c0]) for k in range(KC)],
                             [wk, f"H{blks.index((a, b))}"], [pk])
                        K.A([pk], [f"U{ubuf}"],
                            lambda: nc.scalar.activation(out=U[:, ubuf, c2, a - c0:b - c0], in_=pt[:, 0:nb],
                                                         func=AF.Gelu))
                for c2 in range(2):
                    cg = h * 6 + ub * 2 + c2
                    gi = ub * 2 + c2
                    for t0 in range(0, ntc, 3):
                        t1 = min(ntc, t0 + 3)
                        nt_ = t1 - t0
                        pt, pk = K.bank()
                        K.pre(K.pe, ["V", f"WSR{hb}"], [pk])
                        ins = None
                        for tc in range(t0, t1):
                            ins = nc.tensor.matmul(pt[:, (tc - t0) * 128:(tc - t0 + 1) * 128],
                                                   Vv[:, tc, cg * 128:(cg + 1) * 128], WSR[:, hb, tc, :],
                                                   start=True, stop=True)
                        tk = K.pe.done(ins)
                        K.post(tk, ["V", f"WSR{hb}"], [pk])
                        gb = K.ps_i % 2
                        for tc in range(t0, t1):
                            K.V([pk, "BB"], [f"GT{gb}"],
                                lambda: nc.vector.scalar_tensor_tensor(
                                    out=GT[:, gb, (tc - t0) * 128:(tc - t0 + 1) * 128],
                                    in0=pt[:, (tc - t0) * 128:(tc - t0 + 1) * 128], scalar=VG[:, j, cg:cg + 1],
                                    in1=BB[:, h, :], op0=ALU.mult, op1=ALU.add))
                        K.V([f"GT{gb}", f"U{ubuf}"], [f"G{hb}"],
                            lambda: nc.vector.tensor_tensor(out=G[:, hb, gi, t0 * 128:t1 * 128], in0=GT[:, gb, 0:nt_ * 128],
                                                            in1=U[:, ubuf, c2, t0 * 128:t1 * 128], op=ALU.mult))
            if h >= 1:
                sgu_wout(h - 1)
        sgu_wout(7)

    for t, (a0, a1) in enumerate(TILES):
        for eng in (K.pe, K.act, K.dve):
            pass
        nA = a1 - a0
        cA0, cA1 = 128, 128 + 128 * nA
        tokA = 128 * a0
        K.sp_dma(X[:, :, cA0:cA1], xTv[:, :, tokA + PADL: tokA + 128 * nA + PADL], [], ["X"])
        if t == 0:
            K.sp_dma(XH0[:, :, :].rearrange("p k c -> p (k c)"), xh[:, 0, :], [], ["XH0"])
        pool_layer(0, 0, cA0, cA1, tokA, (XH0, "XH0"))
        if t + 1 < len(TILES):
            K.sp_dma(XH0[:, :, :].rearrange("p k c -> p (k c)"), xh[:, t + 1, :], [], ["XH0"])
        K.barrier()
        mlp(0, cA0, cA1)
        K.barrier()
        sgu_layer(1, 0, cA0, cA1)
        K.barrier()
        mlp(1, cA0, cA1)
        K.barrier()
        if t == 0:
            cB0, nB = 256, nA - 2
            tokB = 0
            lsrc = X[:, :, 248:256]
        else:
            cB0, nB = 0, nA
            tokB = 128 * (a0 - 1)
            lsrc = XL[:, (t - 1) % 2, :, :]
        cB1 = cB0 + 128 * nB
        K.V(["X", "XL"], ["XH"], lambda: nc.vector.tensor_copy(out=XH[:, :, 0:8], in_=lsrc))
        K.V(["X"], ["XH"], lambda: nc.vector.tensor_copy(out=XH[:, :, 8:16], in_=X[:, :, cB1:cB1 + 8]))
        K.V(["X"], ["XL"], lambda: nc.vector.tensor_copy(out=XL[:, t % 2, :, :], in_=X[:, :, cB1 - 8:cB1]))
        pool_layer(2, 1, cB0, cB1, tokB, (XH, "XH"))
        K.barrier()
        mlp(2, cB0, cB1)
        K.barrier()
        sgu_layer(3, 1, cB0, cB1)
        K.barrier()
        mlp(3, cB0, cB1)
        K.barrier()
        fblks = nblocks(cB0, cB1)
        fitems = [(X[:, :, a:b], b - a, NBUF[i]) for i, (a, b) in enumerate(fblks)]
        rstd_blocks(fitems, ["X"])
        for (a, b), (src, nb, B) in zip(fblks, fitems):
            for k in range(KC):
                K.V(["X", B[4]], ["X"],
                    lambda: nc.vector.scalar_tensor_tensor(out=X[:, k, a:b], in0=X[:, k, a:b], scalar=FG[:, k:k + 1],
                                                           in1=B[1][:, 0:nb], op0=ALU.mult, op1=ALU.mult))
        K.barrier()
        out_tk = K.sp_dma(yTv[:, :, tokB:tokB + 128 * nB], X[:, :, cB0:cB1], ["X"], ["Y"])
        if t + 1 < len(TILES):
            K.barrier()
            K.sp.wait(out_tk)
            K.dve.wait(out_tk)
            K.V(["X"], ["X"], lambda: nc.vector.tensor_copy(out=X[:, :, 0:128], in_=X[:, :, 128 * nA:128 * nA + 128]))
            K.barrier()
        else:
            K.sp.wait(out_tk)
    return nc


def _layout_inputs(inp):
    f32 = np.float32
    xs = [(inp["x_prompt"][0], inp["c_prompt"][0]), (inp["x_prompt"][1], inp["c_prompt"][1]),
          (inp["x_sample"][0], inp["c_sample"][0])]
    cores = [(0, 0), (0, 4096), (1, 0), (1, 4096), (2, 0), (2, 4096), (2, 8192), (2, 12288)]

    def col(v):
        v = np.asarray(v, f32)
        lead = v.shape[:-1]
        n = v.shape[-1] // 128
        r = v.reshape(lead + (n, 128))
        return np.ascontiguousarray(np.moveaxis(r, -1, 0))

    shared = {
        "n1": col(inp["norm1_g"]), "n2": col(inp["norm2_g"]), "mb": col(inp["mod_b"]),
        "psc": col(inp["pool_scale"]), "vg": col(inp["sgu_v_gain"]), "fg": col(inp["final_g"]),
        "wst": np.ascontiguousarray(np.transpose(np.asarray(inp["sgu_w_s"], f32), (3, 0, 1, 2))),
        "bbd": np.ascontiguousarray(np.broadcast_to(np.asarray(inp["sgu_b_s"], f32)[None], (128, 2, 8, 128))),
    }
    for k in ("mod_w", "pool_w_in", "pool_w_grp", "pool_w_out", "sgu_w_in", "sgu_w_out", "mlp_w1", "mlp_w2"):
        shared[k] = np.ascontiguousarray(np.asarray(inp[k], f32))
    maps = []
    for (si, s0) in cores:
        x, c = xs[si]
        x = np.asarray(x, f32)
        S = x.shape[0]
        lo, hi = s0 - PADL, s0 + TOK + PADL
        xt = np.zeros((D, XT), f32)
        vlo, vhi = max(lo, 0), min(hi, S)
        xt[:, vlo - lo:vhi - lo] = x[vlo:vhi].T
        pos = np.arange(lo, hi)
        valid = (pos >= 0) & (pos < S)
        m = np.zeros((5, XT), f32)
        m[0] = valid
        for g, w in enumerate((2, 4, 8, 16)):
            lo_ = np.maximum(pos - w // 2, 0)
            hi_ = np.minimum(pos + w // 2 - 1, S - 1) + 1
            cnt = np.where(valid, hi_ - lo_, w)
            m[1 + g] = 1.0 / cnt
        d = dict(shared)
        xr = xt.reshape(KC, 128, XT)
        xhh = np.zeros((128, len(TILES), KC, 16), f32)
        for ti, (a0, a1) in enumerate(TILES):
            xhh[:, ti, :, 0:8] = np.transpose(xr[:, :, 128 * a0 - 8 + PADL:128 * a0 + PADL], (1, 0, 2))
            xhh[:, ti, :, 8:16] = np.transpose(xr[:, :, 128 * a1 + PADL:128 * a1 + 8 + PADL], (1, 0, 2))
        d["xh"] = np.ascontiguousarray(xhh.reshape(128, len(TILES), KC * 16))
        d["xT"] = xt
        d["mr"] = np.ascontiguousarray(np.broadcast_to(m[None], (128, 5, XT)))
        d["cc"] = col(np.asarray(c, f32))
        maps.append(d)
    return maps


_NC = None


def kernel(**inp):
    global _NC
    maps = _layout_inputs(inp)
    if _NC is None:
        _NC = build()
    res = run_bass_kernel_spmd(_NC, maps, core_ids=list(range(NCORE)))
    ys = [np.ascontiguousarray(r["yT"].T) for r in res.results]
    y_prompt = np.stack([np.concatenate(ys[0:2], 0), np.concatenate(ys[2:4], 0)], 0).astype(np.float32)
    y_sample = np.concatenate(ys[4:8], 0)[None].astype(np.float32)
    return (y_prompt, y_sample)
```

```python
import numpy as np
import concourse.bass as bass
import concourse.mybir as mybir
from concourse.bass_utils import run_bass_kernel_spmd

F32 = mybir.dt.float32
BF16 = mybir.dt.bfloat16
AF = mybir.ActivationFunctionType
ALU = mybir.AluOpType
AX = mybir.AxisListType

D = 2048
KC = 16
DEPTH = 4
EPS = 1e-6
NCORE = 8
TOK = 4096
PADL = 136
XT = TOK + 2 * PADL
NCH = 5
TILES = [(-1, 4), (4, 9), (9, 14), (14, 19), (19, 24), (24, 29), (29, 33)]
NSLOT = 4
NXSLOT = 7
HID = 8192
SF = 12288
SH = 6144


class Eng:
    def __init__(self, K, e, name):
        self.K, self.e, self.name = K, e, name
        self.gen = 0
        self.waited = {}
        self.new_sem()

    def new_sem(self):
        self.sem = self.K.sem(f"{self.name}{self.gen}")
        self.gen += 1
        self.n = 0

    def wait(self, tk):
        sem, val, nm = tk
        if self.waited.get(nm, 0) >= val:
            return
        self.e.wait_ge(sem, val)
        self.waited[nm] = val

    def done(self, ins):
        self.n += 1
        ins.then_inc(self.sem, 1)
        return (self.sem, self.n, self.sem_name())

    def sem_name(self):
        return f"{self.name}{self.gen - 1}"


class Kern:
    def __init__(self):
        self.nc = bass.Bass("TRN2", target_bir_lowering=False)
        self._keep = []
        self.W = {}
        self.R = {}
        nc = self.nc
        self.pe = Eng(self, nc.tensor, "pe")
        self.act = Eng(self, nc.scalar, "act")
        self.dve = Eng(self, nc.vector, "dve")
        self.pool = Eng(self, nc.gpsimd, "pool")
        self.sp = Eng(self, nc.sync, "sp")
        self.own = {}
        self.sp_sems = [[self.sem(f"spd{i}"), 0, f"spd{i}", None] for i in range(8)]
        self.sp_i = 0
        self.slot_sems = [[self.sem(f"slot{i}"), 0, f"slot{i}"] for i in range(NSLOT + NXSLOT)]
        self.slot_i = 0
        self.nslot_active = NSLOT
        self.mod_hook = None
        self.in_hook = False
        self.ps_i = 0
        self.last = {}

    def sem(self, name):
        g = self.nc.semaphore(name)
        s = g.__enter__()
        self._keep.append(g)
        return s

    def sb(self, name, shape, dt):
        g = self.nc.sbuf_tensor(name, shape, dt)
        t = g.__enter__()
        self._keep.append(g)
        return t

    def psum(self, name, shape, dt):
        g = self.nc.psum_tensor(name, shape, dt)
        t = g.__enter__()
        self._keep.append(g)
        return t

    def dram(self, name, shape, kind="ExternalInput"):
        return self.nc.dram_tensor(name, list(shape), F32, kind=kind).ap()

    def pre(self, eng, reads, writes):
        for k in reads:
            for tk in self.W.get(k, {}).values():
                eng.wait(tk)
        for k in writes:
            for tk in self.W.get(k, {}).values():
                if tk[2].startswith(eng.name) and eng.name in ("pe", "act", "dve"):
                    continue
                eng.wait(tk)
            for tk in self.R.get(k, {}).values():
                if tk[2].startswith(eng.name) and eng.name in ("pe", "act", "dve"):
                    continue
                eng.wait(tk)

    def post(self, tk, reads, writes):
        for k in reads:
            self.R.setdefault(k, {})[tk[2]] = tk
        for k in writes:
            self.W[k] = {tk[2]: tk}
            self.R[k] = {}
        self.last[tk[2]] = tk

    def op(self, eng, reads, writes, fn):
        self.pre(eng, reads, writes)
        tk = eng.done(fn())
        self.post(tk, reads, writes)
        return tk

    def A(self, reads, writes, fn):
        return self.op(self.act, reads, writes, fn)

    def V(self, reads, writes, fn):
        return self.op(self.dve, reads, writes, fn)

    def barrier(self):
        tks = [tk for nm, tk in self.last.items() if nm[:2] in ("pe", "ac", "dv")]
        for eng in (self.pe, self.act, self.dve):
            for tk in tks:
                if tk[2] == eng.sem_name():
                    continue
                eng.wait(tk)
        self.W["__all__"] = {tk[2]: tk for tk in tks}

    def sp_dma(self, out, in_, reads, writes):
        reads = list(reads)
        ent = self.sp_sems[self.sp_i]
        self.sp_i = (self.sp_i + 1) % len(self.sp_sems)
        if ent[3] is not None:
            self.sp.wait(ent[3])
        self.pre(self.sp, list(reads) + ["__all__"], writes)
        ent[1] += 16
        self.nc.sync.dma_start(out=out, in_=in_).then_inc(ent[0], 16)
        tk = (ent[0], ent[1], ent[2])
        ent[3] = tk
        self.post(tk, reads, writes)
        return tk

    def wload(self, shape, src):
        if self.mod_hook is not None and not self.in_hook:
            self.in_hook = True
            self.mod_hook()
            self.in_hook = False
        i = self.slot_i
        self.slot_i = (self.slot_i + 1) % self.nslot_active
        ent = self.slot_sems[i]
        key = f"ring{i}"
        self.pre(self.pool, [], [key])
        n = 1
        for s in shape[1:]:
            n *= s
        view = self.slot_views[i][:, 0:n]
        if len(shape) == 3:
            view = view.rearrange("p (a b) -> p a b", b=shape[2])
        ent[1] += 16
        self.nc.gpsimd.dma_start(out=view, in_=src).then_inc(ent[0], 16)
        tk = (ent[0], ent[1], ent[2])
        self.post(tk, [], [key])
        return view, key

    def mm(self, out, pairs, reads, writes):
        self.pre(self.pe, reads, writes)
        n = len(pairs)
        ins = None
        for i, (l, r) in enumerate(pairs):
            ins = self.nc.tensor.matmul(out, l, r, start=(i == 0), stop=(i == n - 1))
        tk = self.pe.done(ins)
        self.post(tk, reads, writes)
        return tk

    def bank(self):
        i = self.ps_i
        self.ps_i = (self.ps_i + 1) % 7
        return self.ps[i], f"ps{i}"


def nblocks(c0, c1):
    n = (c1 - c0) // 128
    nb = (n + 2) // 3
    base, rem = divmod(n, nb)
    out = []
    c = c0
    for i in range(nb):
        w = (base + (1 if i >= nb - rem else 0)) * 128
        out.append((c, c + w))
        c += w
    return out


def build():
    K = Kern()
    nc = K.nc
    xT = K.dram("xT", [D, XT])
    mr = K.dram("mr", [128, 5, XT])
    xh = K.dram("xh", [128, len(TILES), KC * 16])
    cc = K.dram("cc", [128, KC])
    n1 = K.dram("n1", [128, DEPTH, KC])
    n2 = K.dram("n2", [128, DEPTH, KC])
    mb = K.dram("mb", [128, DEPTH, 96])
    psc = K.dram("psc", [128, 2, KC])
    vg = K.dram("vg", [128, 2, 48])
    fg = K.dram("fg", [128, KC])
    wst = K.dram("wst", [128, 2, 8, 128])
    bbd = K.dram("bbd", [128, 2, 8, 128])
    mod_w = K.dram("mod_w", [DEPTH, D, 6 * D])
    pool_w_in = K.dram("pool_w_in", [2, D, D])
    pool_w_grp = K.dram("pool_w_grp", [2, 4, 512, 512])
    pool_w_out = K.dram("pool_w_out", [2, D, D])
    sgu_w_in = K.dram("sgu_w_in", [2, D, SF])
    sgu_w_out = K.dram("sgu_w_out", [2, SH, D])
    mlp_w1 = K.dram("mlp_w1", [DEPTH, D, HID])
    mlp_w2 = K.dram("mlp_w2", [DEPTH, HID, D])
    yT = K.dram("yT", [D, TOK], kind="ExternalOutput")
    xTv = xT.rearrange("(k p) t -> p k t", p=128)
    yTv = yT.rearrange("(k p) t -> p k t", p=128)

    WMAX = NCH * 128
    X = K.sb("X", [128, KC, (NCH + 1) * 128], F32)
    H = K.sb("H", [128, KC, WMAX], BF16)
    Hh = K.sb("Hh", [128, KC, 16], BF16)
    XH = K.sb("XH", [128, KC, 16], F32)
    XH0 = K.sb("XH0", [128, KC, 16], F32)
    XL = K.sb("XL", [128, 2, KC, 8], F32)
    BIG = K.sb("BIG", [128, NCH * SH // 2], F32)
    BIGb = BIG[:, :].bitcast(BF16)
    Vv = BIGb.rearrange("p (t c) -> p t c", c=SH)
    Abuf = BIGb[:, 0:KC * WMAX].rearrange("p (k w) -> p k w", w=WMAX)
    SQ = BIGb[:, 10240:10240 + KC * 384].rearrange("p (k w) -> p k w", w=384)
    LZ = WMAX + 16
    NT = BIG[:, 8192:8960].rearrange("p (a b) -> p a b", b=384)
    RS = BIG[:, 8960:9344]
    SR = BIG[:, 9344:9728]
    RT = BIG[:, 9728:10496].rearrange("p (a b) -> p a b", b=384)
    ZE = BIG[:, 9728:9728 + LZ]
    T1 = BIG[:, 10384:10384 + LZ]
    T2 = BIG[:, 11040:11040 + LZ]
    MR = BIG[:, 11696:11696 + 5 * LZ].rearrange("p (a b) -> p a b", b=LZ)
    G = K.sb("G", [128, 2, 6, WMAX], BF16)
    U = K.sb("U", [128, 2, 2, WMAX], BF16)
    WSR = K.sb("WSR", [128, 2, NCH, 128], BF16)
    GT = K.sb("GT", [128, 2, 384], F32)
    BB = K.sb("BB", [128, 8, 128], F32)
    WST = K.sb("WST", [128, 8, 128], F32)
    SSQ = K.sb("SSQ", [128, NCH, 24], F32)
    SQV = K.sb("SQV", [128, 2, 256], F32)
    RV = K.sb("RV", [128, 2, NCH], F32)
    K.ring = K.sb("ring", [128, NSLOT, 4096], BF16)
    K.slot_views = [K.ring[:, i, :] for i in range(NSLOT)] + [BIGb[:, i * 4096:(i + 1) * 4096] for i in range(NXSLOT)]
    SQh = K.sb("SQh", [128, KC, 16], BF16)
    RSh = K.sb("RSh", [128, 2, 16], F32)
    ONES = K.sb("ONES", [128, 128], BF16)
    CA = K.sb("CA", [128, KC], F32)
    CAb = K.sb("CAb", [128, KC], BF16)
    MODC = K.sb("MODC", [128, DEPTH, 96], F32)
    MBt = K.sb("MBt", [128, DEPTH, 96], F32)
    N1 = K.sb("N1", [128, DEPTH, KC], F32)
    N2 = K.sb("N2", [128, DEPTH, KC], F32)
    GM = K.sb("GM", [128, DEPTH, 2, KC], F32)
    PSC = K.sb("PSC", [128, 2, KC], F32)
    VG = K.sb("VG", [128, 2, 48], F32)
    FG = K.sb("FG", [128, KC], F32)
    K.ps = [K.psum(f"ps{i}", [128, 512], F32) for i in range(8)]

    SQD = float(np.sqrt(D))

    K.sp_dma(CA[:], cc, [], ["CA"])
    K.sp_dma(MBt[:], mb, [], ["MBt"])
    K.sp_dma(N1[:], n1, [], ["N1"])
    K.sp_dma(N2[:], n2, [], ["N2"])
    K.sp_dma(PSC[:], psc, [], ["PSC"])
    K.sp_dma(VG[:], vg, [], ["VG"])
    K.sp_dma(FG[:], fg, [], ["FG"])
    K.V([], ["ONES"], lambda: nc.vector.memset(ONES[:], 1.0))
    K.A(["CA"], ["CAb"], lambda: nc.scalar.activation(out=CAb[:], in_=CA[:], func=AF.Silu))
    ptM, pkM = K.ps[7], "psM"

    def mod_block(l, b):
        mwv = mod_w[l].rearrange("(k p) e -> p k e", p=128)
        wv, wk = K.wload([128, KC, 256], mwv[:, :, b * 256:(b + 1) * 256])
        for e2 in range(2):
            j = b * 2 + e2
            K.mm(ptM[:, j:j + 1],
                 [(wv[:, k, e2 * 128:(e2 + 1) * 128], CAb[:, k:k + 1]) for k in range(KC)],
                 [wk, "CAb"], [pkM])
        if b == 47:
            K.V([pkM, "MBt"], [f"MODC{l}"],
                lambda: nc.vector.tensor_tensor(out=MODC[:, l, :], in0=ptM[:, 0:96], in1=MBt[:, l, :], op=ALU.add))
            for s_, Nn in ((0, N1), (1, N2)):
                sc = MODC[:, l, (3 * s_ + 1) * KC:(3 * s_ + 2) * KC]
                K.V([f"MODC{l}"], [f"GMt{l}{s_}"],
                    lambda: nc.vector.tensor_scalar(out=GM[:, l, s_, :], in0=sc, scalar1=1.0, scalar2=SQD,
                                                    op0=ALU.add, op1=ALU.mult))
                K.V([f"GMt{l}{s_}", "N1", "N2"], [f"GM{l}{s_}"],
                    lambda: nc.vector.tensor_tensor(out=GM[:, l, s_, :], in0=GM[:, l, s_, :], in1=Nn[:, l, :],
                                                    op=ALU.mult))

    K.nslot_active = NSLOT + NXSLOT
    for b in range(48):
        mod_block(0, b)
    K.V(["FG"], ["FGs"], lambda: nc.vector.tensor_scalar(out=FG[:], in0=FG[:], scalar1=SQD, scalar2=None,
                                                         op0=ALU.mult))
    K.barrier()
    K.nslot_active = NSLOT
    K.slot_i = 0
    pending = [(l, b) for l in range(1, DEPTH) for b in range(48)]

    def hook():
        if pending:
            l, b = pending.pop(0)
            mod_block(l, b)

    K.mod_hook = hook

    def modv(l, idx):
        return MODC[:, l, idx * KC:(idx + 1) * KC]

    SQ2 = BIGb[:, 23392:23392 + KC * 256].rearrange("p (k w) -> p k w", w=256)
    RS2 = BIG[:, 13744:14000]
    SR2 = BIG[:, 14000:14256]
    NBUF = [(SQ, RS, SR, "SQ", "RS", "SR"), (SQ2, RS2, SR2, "SQ2", "RS2", "SR2")]
    HBUF = (SQh, RSh[:, 0, :], RSh[:, 1, :], "SQh", "RSh", "SRh")

    def rstd_blocks(items, key_src):
        for (src, nb, B) in items:
            if B[3] == "SQ2":
                K.V(key_src, [B[3]], lambda: nc.vector.tensor_tensor(out=B[0][:, :, 0:nb], in0=src, in1=src,
                                                                     op=ALU.mult))
            else:
                K.A(key_src, [B[3]], lambda: nc.scalar.activation(out=B[0][:, :, 0:nb], in_=src, func=AF.Square))
        banks = []
        for (src, nb, B) in items:
            pt, pk = K.bank()
            K.mm(pt[:, 0:nb], [(ONES[:], B[0][:, k, 0:nb]) for k in range(KC)], [B[3], "ONES"], [pk])
            banks.append((pt, pk))
        for (src, nb, B), (pt, pk) in zip(items, banks):
            K.A([pk], [B[5]], lambda: nc.scalar.activation(out=B[2][:, 0:nb], in_=pt[:, 0:nb], func=AF.Sqrt,
                                                           bias=float(D * EPS), scale=1.0))
            K.V([B[5]], [B[4]], lambda: nc.vector.reciprocal(out=B[1][:, 0:nb], in_=B[2][:, 0:nb]))

    def modulate(src, dst, nb, B, gm, sh, key_src, key_dst):
        for k in range(KC):
            tb = k % 2
            K.V(key_src + [B[4]], [f"NT{tb}"],
                lambda: nc.vector.tensor_tensor(out=NT[:, tb, 0:nb], in0=src[:, k, :], in1=B[1][:, 0:nb],
                                                op=ALU.mult))
            K.A([f"NT{tb}"], key_dst,
                lambda: nc.scalar.activation(out=dst[:, k, :], in_=NT[:, tb, 0:nb], func=AF.Identity,
                                             bias=sh[:, k:k + 1], scale=gm[:, k:k + 1]))

    def norm_range(c0, c1, gm, sh, halo=None):
        blks = nblocks(c0, c1)
        items = [(X[:, :, a:b], b - a, NBUF[len(blks) - 1 - i]) for i, (a, b) in enumerate(blks)]
        if halo is not None:
            rstd_blocks([(halo[0][:, :, :], 16, HBUF)], [halo[1]])
        rstd_blocks(items, ["X"])
        if halo is not None:
            modulate(halo[0][:, :, :], Hh[:, :, :], 16, HBUF, gm, sh, [halo[1]], ["Hh"])
        for (a, b), (src, nb, B) in zip(blks, items):
            modulate(src, H[:, :, a - c0:b - c0], nb, B, gm, sh, ["X"], [f"H{blks.index((a, b))}"])

    def resid_update(pt, pk, e, a, b, gate):
        K.V([pk, "X"], ["X"],
            lambda: nc.vector.scalar_tensor_tensor(out=X[:, e, a:b], in0=pt[:, 0:b - a], scalar=gate[:, e:e + 1],
                                                   in1=X[:, e, a:b], op0=ALU.mult, op1=ALU.add))

    def mlp(l, c0, c1):
        W = c1 - c0
        norm_range(c0, c1, GM[:, l, 1, :], modv(l, 3))
        g2 = modv(l, 5)
        blks = nblocks(c0, c1)
        w1v = mlp_w1[l].rearrange("(k p) f -> p k f", p=128)
        for s in range(4):
            for b in range(8):
                f0 = s * 2048 + b * 256
                wv, wk = K.wload([128, KC, 256], w1v[:, :, f0:f0 + 256])
                order = [(f2, ab) for f2 in range(2) for ab in blks]
                if s == 0 and b == 0:
                    order = [(f2, ab) for ab in blks for f2 in range(2)]
                for (f2, (a, bb_)) in order:
                    fi = b * 2 + f2
                    nb = bb_ - a
                    pt, pk = K.bank()
                    K.mm(pt[:, 0:nb],
                         [(wv[:, k, f2 * 128:(f2 + 1) * 128], H[:, k, a - c0:bb_ - c0]) for k in range(KC)],
                         [wk, f"H{blks.index((a, bb_))}"], [pk])
                    tb = K.ps_i % 2
                    K.A([pk], [f"RT{tb}"],
                        lambda: nc.scalar.activation(out=RT[:, tb, 0:nb], in_=pt[:, 0:nb], func=AF.Relu))
                    K.V([f"RT{tb}"], ["A"],
                        lambda: nc.vector.tensor_tensor(out=Abuf[:, fi, a - c0:bb_ - c0], in0=RT[:, tb, 0:nb],
                                                        in1=RT[:, tb, 0:nb], op=ALU.mult))
            w2v = mlp_w2[l][s * 2048:(s + 1) * 2048, :].rearrange("(f p) e -> p f e", p=128)
            for b in range(8):
                wv, wk = K.wload([128, KC, 256], w2v[:, :, b * 256:(b + 1) * 256])
                for e2 in range(2):
                    e = b * 2 + e2
                    for (a, bb_) in blks:
                        nb = bb_ - a
                        pt, pk = K.bank()
                        K.mm(pt[:, 0:nb],
                             [(wv[:, f, e2 * 128:(e2 + 1) * 128], Abuf[:, f, a - c0:bb_ - c0]) for f in range(KC)],
                             [wk, "A"], [pk])
                        resid_update(pt, pk, e, a, bb_, g2)

    def pool_layer(l, j, c0, c1, tok0, halo):
        W = c1 - c0
        L = W + 16
        gm, sh, g1 = GM[:, l, 0, :], modv(l, 0), modv(l, 2)
        norm_range(c0, c1, gm, sh, halo=halo)
        K.sp_dma(MR[:, :, 0:L], mr[:, :, tok0 - 8 + PADL: tok0 + W + 8 + PADL], [], ["MR", "SQ2", "RS2", "SR2"])
        blks = nblocks(c0, c1)
        wiv = pool_w_in[j].rearrange("(k p) e -> p k e", p=128)
        for e in reversed(range(KC)):
            if e % 2 == 1:
                wv, wk = K.wload([128, KC, 256], wiv[:, :, (e - 1) * 128:(e + 1) * 128])
            e2 = e % 2
            for (a, b) in blks:
                nb = b - a
                pt, pk = K.bank()
                K.mm(pt[:, 0:nb], [(wv[:, k, e2 * 128:(e2 + 1) * 128], H[:, k, a - c0:b - c0]) for k in range(KC)],
                     [wk, f"H{blks.index((a, b))}"], [pk])
                K.A([pk], ["ZE"], lambda: nc.scalar.copy(out=ZE[:, 8 + a - c0: 8 + b - c0], in_=pt[:, 0:nb]))
            pt, pk = K.bank()
            K.mm(pt[:, 0:16], [(wv[:, k, e2 * 128:(e2 + 1) * 128], Hh[:, k, :]) for k in range(KC)],
                 [wk, "Hh"], [pk])
            K.A([pk], ["ZE"], lambda: nc.scalar.copy(out=ZE[:, 0:8], in_=pt[:, 0:8]))
            K.A([pk], ["ZE"], lambda: nc.scalar.copy(out=ZE[:, 8 + W:16 + W], in_=pt[:, 8:16]))
            K.V(["ZE", "MR"], ["ZE"],
                lambda: nc.vector.tensor_tensor(out=ZE[:, 0:L], in0=ZE[:, 0:L], in1=MR[:, 0, 0:L], op=ALU.mult))
            g = e // 4
            src, srck = ZE, "ZE"
            bufs = [(T1, "T1"), (T2, "T2")]
            for st in range(g + 1):
                sh_ = 1 << st
                dst, dstk = bufs[st % 2]
                n = L - (2 * sh_ - 1)
                K.V([srck], [dstk],
                    lambda: nc.vector.tensor_tensor(out=dst[:, 0:n], in0=src[:, 0:n], in1=src[:, sh_:sh_ + n],
                                                    op=ALU.add))
                src, srck = dst, dstk
            T3, t3k = bufs[(g + 1) % 2]
            hw = (2 << g) // 2
            o = 8 - hw
            K.V([srck, "MR"], [t3k],
                lambda: nc.vector.tensor_tensor(out=T3[:, 0:W], in0=src[:, o:o + W], in1=MR[:, 1 + g, 8:8 + W],
                                                op=ALU.mult))
            K.V([t3k, "ZE"], ["A"],
                lambda: nc.vector.tensor_tensor(out=Abuf[:, e, 0:W], in0=T3[:, 0:W], in1=ZE[:, 8:8 + W],
                                                op=ALU.subtract))
        for g in range(4):
            wgv = pool_w_grp[j, g].rearrange("(k p) e -> p k e", p=128)
            wv, wk = K.wload([128, 4, 512], wgv)
            for eo in range(4):
                e = 4 * g + eo
                for (a, b) in blks:
                    nb = b - a
                    pt, pk = K.bank()
                    K.mm(pt[:, 0:nb],
                         [(wv[:, ki, eo * 128:(eo + 1) * 128], Abuf[:, 4 * g + ki, a - c0:b - c0]) for ki in range(4)],
                         [wk, "A"], [pk])
                    K.A([pk, f"H{blks.index((a, b))}"], [f"H{blks.index((a, b))}"],
                        lambda: nc.scalar.activation(out=H[:, e, a - c0:b - c0], in_=pt[:, 0:nb], func=AF.Identity,
                                                     bias=0.0, scale=PSC[:, j, e:e + 1]))
        wov = pool_w_out[j].rearrange("(k p) e -> p k e", p=128)
        for eb in range(8):
            wv, wk = K.wload([128, KC, 256], wov[:, :, eb * 256:(eb + 1) * 256])
            for e2 in range(2):
                e = eb * 2 + e2
                for (a, b) in blks:
                    nb = b - a
                    pt, pk = K.bank()
                    K.mm(pt[:, 0:nb],
                         [(wv[:, k, e2 * 128:(e2 + 1) * 128], H[:, k, a - c0:b - c0]) for k in range(KC)],
                         [wk, f"H{blks.index((a, b))}"], [pk])
                    resid_update(pt, pk, e, a, b, g1)

    def sgu_layer(l, j, c0, c1):
        W = c1 - c0
        ntc = W // 128
        gm, sh, g1 = GM[:, l, 0, :], modv(l, 0), modv(l, 2)
        K.sp_dma(WST[:], wst[:, j], [], ["WST"])
        K.sp_dma(BB[:], bbd[:, j], [], ["BB"])
        norm_range(c0, c1, gm, sh)
        K.barrier()
        blks = nblocks(c0, c1)
        wiv = sgu_w_in[j].rearrange("(k p) f -> p k f", p=128)
        for vb in range(24):
            wv, wk = K.wload([128, KC, 256], wiv[:, :, SH + vb * 256: SH + (vb + 1) * 256])
            for tc in range(ntc):
                pt, pk = K.bank()
                K.mm(pt[:, 0:256], [(H[:, k, tc * 128:(tc + 1) * 128], wv[:, k, :]) for k in range(KC)],
                     [wk, f"H{[i for i, (ba, bb2) in enumerate(blks) if ba <= c0 + tc * 128 < bb2][0]}"], [pk])
                K.A([pk], ["V"], lambda: nc.scalar.activation(out=Vv[:, tc, vb * 256:(vb + 1) * 256],
                                                              in_=pt[:, 0:256], func=AF.Gelu))
                tb = (vb * ntc + tc) % 2
                K.V(["V"], [f"SQV{tb}"],
                    lambda: nc.vector.tensor_tensor(out=SQV[:, tb, :], in0=Vv[:, tc, vb * 256:(vb + 1) * 256],
                                                    in1=Vv[:, tc, vb * 256:(vb + 1) * 256], op=ALU.mult))
                K.V([f"SQV{tb}"], ["SSQ"],
                    lambda: nc.vector.reduce_sum(out=SSQ[:, tc, vb:vb + 1], in_=SQV[:, tb, :], axis=AX.X))
        K.V(["SSQ"], ["RV0"], lambda: nc.vector.reduce_sum(out=RV[:, 0, 0:ntc], in_=SSQ[:, 0:ntc, :], axis=AX.X))
        K.A(["RV0"], ["RV1"], lambda: nc.scalar.activation(out=RV[:, 1, 0:ntc], in_=RV[:, 0, 0:ntc], func=AF.Sqrt,
                                                           bias=float(EPS), scale=1.0 / SH))
        K.V(["RV1"], ["RV"], lambda: nc.vector.reciprocal(out=RV[:, 0, 0:ntc], in_=RV[:, 1, 0:ntc]))
        wov = sgu_w_out[j].rearrange("(f p) e -> p f e", p=128)
        ui = 0

        def sgu_wout(hh):
            for ob in range(4):
                wv, wk = K.wload([128, 6, 512], wov[:, hh * 6:(hh + 1) * 6, ob * 512:(ob + 1) * 512])
                for e4 in range(4):
                    e = ob * 4 + e4
                    for (a, b) in blks:
                        nb = b - a
                        pt, pk = K.bank()
                        K.mm(pt[:, 0:nb],
                             [(wv[:, f, e4 * 128:(e4 + 1) * 128], G[:, hh % 2, f, a - c0:b - c0]) for f in range(6)],
                             [wk, f"G{hh % 2}"], [pk])
                        resid_update(pt, pk, e, a, b, g1)

        for h in range(8):
            hb = h % 2
            for tc in range(ntc):
                K.V(["RV", "WST"], [f"WSR{hb}"],
                    lambda: nc.vector.tensor_scalar(out=WSR[:, hb, tc, :], in0=WST[:, h, :],
                                                    scalar1=RV[:, 0, tc:tc + 1], scalar2=None, op0=ALU.mult))
            for ub in range(3):
                col0 = h * 768 + ub * 256
                wv, wk = K.wload([128, KC, 256], wiv[:, :, col0:col0 + 256])
                ubuf = ui % 2
                ui += 1
                for c2 in range(2):
                    for (a, b) in blks:
                        nb = b - a
                        pt, pk = K.bank()
                        K.mm(pt[:, 0:nb],
                             [(wv[:, k, c2 * 128:(c2 + 1) * 128], H[:, k, a - c0:b - c0]) for k in range(KC)],
                             [wk, f"H{blks.index((a, b))}"], [pk])
                        K.A([pk], [f"U{ubuf}"],
                            lambda: nc.scalar.activation(out=U[:, ubuf, c2, a - c0:b - c0], in_=pt[:, 0:nb],
                                                         func=AF.Gelu))
                for c2 in range(2):
                    cg = h * 6 + ub * 2 + c2
                    gi = ub * 2 + c2
                    for t0 in range(0, ntc, 3):
                        t1 = min(ntc, t0 + 3)
                        nt_ = t1 - t0
                        pt, pk = K.bank()
                        K.pre(K.pe, ["V", f"WSR{hb}"], [pk])
                        ins = None
                        for tc in range(t0, t1):
                            ins = nc.tensor.matmul(pt[:, (tc - t0) * 128:(tc - t0 + 1) * 128],
                                                   Vv[:, tc, cg * 128:(cg + 1) * 128], WSR[:, hb, tc, :],
                                                   start=True, stop=True)
                        tk = K.pe.done(ins)
                        K.post(tk, ["V", f"WSR{hb}"], [pk])
                        gb = K.ps_i % 2
                        for tc in range(t0, t1):
                            K.V([pk, "BB"], [f"GT{gb}"],
                                lambda: nc.vector.scalar_tensor_tensor(
                                    out=GT[:, gb, (tc - t0) * 128:(tc - t0 + 1) * 128],
                                    in0=pt[:, (tc - t0) * 128:(tc - t0 + 1) * 128], scalar=VG[:, j, cg:cg + 1],
                                    in1=BB[:, h, :], op0=ALU.mult, op1=ALU.add))
                        K.V([f"GT{gb}", f"U{ubuf}"], [f"G{hb}"],
                            lambda: nc.vector.tensor_tensor(out=G[:, hb, gi, t0 * 128:t1 * 128], in0=GT[:, gb, 0:nt_ * 128],
                                                            in1=U[:, ubuf, c2, t0 * 128:t1 * 128], op=ALU.mult))
            if h >= 1:
                sgu_wout(h - 1)
        sgu_wout(7)

    for t, (a0, a1) in enumerate(TILES):
        for eng in (K.pe, K.act, K.dve):
            pass
        nA = a1 - a0
        cA0, cA1 = 128, 128 + 128 * nA
        tokA = 128 * a0
        K.sp_dma(X[:, :, cA0:cA1], xTv[:, :, tokA + PADL: tokA + 128 * nA + PADL], [], ["X"])
        if t == 0:
            K.sp_dma(XH0[:, :, :].rearrange("p k c -> p (k c)"), xh[:, 0, :], [], ["XH0"])
        pool_layer(0, 0, cA0, cA1, tokA, (XH0, "XH0"))
        if t + 1 < len(TILES):
            K.sp_dma(XH0[:, :, :].rearrange("p k c -> p (k c)"), xh[:, t + 1, :], [], ["XH0"])
        K.barrier()
        mlp(0, cA0, cA1)
        K.barrier()
        sgu_layer(1, 0, cA0, cA1)
        K.barrier()
        mlp(1, cA0, cA1)
        K.barrier()
        if t == 0:
            cB0, nB = 256, nA - 2
            tokB = 0
            lsrc = X[:, :, 248:256]
        else:
            cB0, nB = 0, nA
            tokB = 128 * (a0 - 1)
            lsrc = XL[:, (t - 1) % 2, :, :]
        cB1 = cB0 + 128 * nB
        K.V(["X", "XL"], ["XH"], lambda: nc.vector.tensor_copy(out=XH[:, :, 0:8], in_=lsrc))
        K.V(["X"], ["XH"], lambda: nc.vector.tensor_copy(out=XH[:, :, 8:16], in_=X[:, :, cB1:cB1 + 8]))
        K.V(["X"], ["XL"], lambda: nc.vector.tensor_copy(out=XL[:, t % 2, :, :], in_=X[:, :, cB1 - 8:cB1]))
        pool_layer(2, 1, cB0, cB1, tokB, (XH, "XH"))
        K.barrier()
        mlp(2, cB0, cB1)
        K.barrier()
        sgu_layer(3, 1, cB0, cB1)
        K.barrier()
        mlp(3, cB0, cB1)
        K.barrier()
        fblks = nblocks(cB0, cB1)
        fitems = [(X[:, :, a:b], b - a, NBUF[len(fblks) - 1 - i]) for i, (a, b) in enumerate(fblks)]
        rstd_blocks(fitems, ["X"])
        for (a, b), (src, nb, B) in zip(fblks, fitems):
            for k in range(KC):
                K.V(["X", B[4]], ["X"],
                    lambda: nc.vector.scalar_tensor_tensor(out=X[:, k, a:b], in0=X[:, k, a:b], scalar=FG[:, k:k + 1],
                                                           in1=B[1][:, 0:nb], op0=ALU.mult, op1=ALU.mult))
        K.barrier()
        out_tk = K.sp_dma(yTv[:, :, tokB:tokB + 128 * nB], X[:, :, cB0:cB1], ["X"], ["Y"])
        if t + 1 < len(TILES):
            K.barrier()
            K.sp.wait(out_tk)
            K.dve.wait(out_tk)
            K.V(["X"], ["X"], lambda: nc.vector.tensor_copy(out=X[:, :, 0:128], in_=X[:, :, 128 * nA:128 * nA + 128]))
            K.barrier()
        else:
            K.sp.wait(out_tk)
    return nc


def _layout_inputs(inp):
    f32 = np.float32
    xs = [(inp["x_prompt"][0], inp["c_prompt"][0]), (inp["x_prompt"][1], inp["c_prompt"][1]),
          (inp["x_sample"][0], inp["c_sample"][0])]
    cores = [(0, 0), (0, 4096), (1, 0), (1, 4096), (2, 0), (2, 4096), (2, 8192), (2, 12288)]

    def col(v):
        v = np.asarray(v, f32)
        lead = v.shape[:-1]
        n = v.shape[-1] // 128
        r = v.reshape(lead + (n, 128))
        return np.ascontiguousarray(np.moveaxis(r, -1, 0))

    shared = {
        "n1": col(inp["norm1_g"]), "n2": col(inp["norm2_g"]), "mb": col(inp["mod_b"]),
        "psc": col(inp["pool_scale"]), "vg": col(inp["sgu_v_gain"]), "fg": col(inp["final_g"]),
        "wst": np.ascontiguousarray(np.transpose(np.asarray(inp["sgu_w_s"], f32), (3, 0, 1, 2))),
        "bbd": np.ascontiguousarray(np.broadcast_to(np.asarray(inp["sgu_b_s"], f32)[None], (128, 2, 8, 128))),
    }
    for k in ("mod_w", "pool_w_in", "pool_w_grp", "pool_w_out", "sgu_w_in", "sgu_w_out", "mlp_w1", "mlp_w2"):
        shared[k] = np.ascontiguousarray(np.asarray(inp[k], f32))
    maps = []
    for (si, s0) in cores:
        x, c = xs[si]
        x = np.asarray(x, f32)
        S = x.shape[0]
        lo, hi = s0 - PADL, s0 + TOK + PADL
        xt = np.zeros((D, XT), f32)
        vlo, vhi = max(lo, 0), min(hi, S)
        xt[:, vlo - lo:vhi - lo] = x[vlo:vhi].T
        pos = np.arange(lo, hi)
        valid = (pos >= 0) & (pos < S)
        m = np.zeros((5, XT), f32)
        m[0] = valid
        for g, w in enumerate((2, 4, 8, 16)):
            lo_ = np.maximum(pos - w // 2, 0)
            hi_ = np.minimum(pos + w // 2 - 1, S - 1) + 1
            cnt = np.where(valid, hi_ - lo_, w)
            m[1 + g] = 1.0 / cnt
        d = dict(shared)
        xr = xt.reshape(KC, 128, XT)
        xhh = np.zeros((128, len(TILES), KC, 16), f32)
        for ti, (a0, a1) in enumerate(TILES):
            xhh[:, ti, :, 0:8] = np.transpose(xr[:, :, 128 * a0 - 8 + PADL:128 * a0 + PADL], (1, 0, 2))
            xhh[:, ti, :, 8:16] = np.transpose(xr[:, :, 128 * a1 + PADL:128 * a1 + 8 + PADL], (1, 0, 2))
        d["xh"] = np.ascontiguousarray(xhh.reshape(128, len(TILES), KC * 16))
        d["xT"] = xt
        d["mr"] = np.ascontiguousarray(np.broadcast_to(m[None], (128, 5, XT)))
        d["cc"] = col(np.asarray(c, f32))
        maps.append(d)
    return maps


_NC = None


def kernel(**inp):
    global _NC
    maps = _layout_inputs(inp)
    if _NC is None:
        _NC = build()
    res = run_bass_kernel_spmd(_NC, maps, core_ids=list(range(NCORE)))
    ys = [np.ascontiguousarray(r["yT"].T) for r in res.results]
    y_prompt = np.stack([np.concatenate(ys[0:2], 0), np.concatenate(ys[2:4], 0)], 0).astype(np.float32)
    y_sample = np.concatenate(ys[4:8], 0)[None].astype(np.float32)
    return (y_prompt, y_sample)
```
